# Optimizing a Trainium2 kernel written in Bass

```python
import jax, jax.numpy as jnp
from jax import lax
import numpy as np

D_MODEL = 4096
BATCH = 8
SEQ = 2048
DEPTH = 4

N_MIXERS = 2
N_LAYERS_A = (DEPTH + N_MIXERS - 1) // N_MIXERS
N_LAYERS_B = DEPTH // N_MIXERS

CHUNK = 128
GMLP_GROUPS = 32
GMLP_WIDTH = D_MODEL
GMLP_GROUP_DIM = GMLP_WIDTH // GMLP_GROUPS

MLA_HEADS = 32
Q_LORA_RANK = 1024
KV_LORA_RANK = 512
QK_NOPE_DIM = 128
QK_ROPE_DIM = 64
V_HEAD_DIM = 128
ROPE_BASE = 10000.0
Q_BLOCK = 128

FFN_HIDDEN = -(-8 * D_MODEL // (3 * 256)) * 256

RMS_EPS = 1e-6
LN_EPS = 1e-5

kernel_name = 'hybrid_gmlp_mla_swiglu_sandwich'


def rms_norm(x, g):
    xf = x.astype(jnp.float32)
    y = xf * lax.rsqrt(jnp.mean(xf * xf, axis=-1, keepdims=True) + RMS_EPS)
    return (y * g.astype(jnp.float32)).astype(x.dtype)


def layer_norm(x, g, b):
    xf = x.astype(jnp.float32)
    mu = jnp.mean(xf, axis=-1, keepdims=True)
    xc = xf - mu
    y = xc * lax.rsqrt(jnp.mean(xc * xc, axis=-1, keepdims=True) + LN_EPS)
    return (y * g.astype(jnp.float32) + b.astype(jnp.float32)).astype(x.dtype)


def rope_tables(positions):
    inv_freq = ROPE_BASE ** (-jnp.arange(0, QK_ROPE_DIM, 2, dtype=jnp.float32) / QK_ROPE_DIM)
    ang = positions.astype(jnp.float32)[..., None] * inv_freq
    return jnp.cos(ang), jnp.sin(ang)


def apply_rope(x, cos, sin):
    xf = x.astype(jnp.float32)
    x1, x2 = jnp.split(xf, 2, axis=-1)
    out = jnp.concatenate([x1 * cos - x2 * sin, x2 * cos + x1 * sin], axis=-1)
    return out.astype(x.dtype)


def gmlp_mixer(h, w_in, ln_g, ln_b, w_s, b_s, w_out):
    B, S, _ = h.shape
    z = jax.nn.gelu(h @ w_in, approximate=False)
    u, v = jnp.split(z, 2, axis=-1)
    v = layer_norm(v, ln_g, ln_b)
    nc = S // CHUNK
    v = v.reshape(B, nc, CHUNK, GMLP_GROUPS, GMLP_GROUP_DIM)
    causal = jnp.tril(jnp.ones((CHUNK, CHUNK), dtype=bool))
    w = jnp.where(causal[None], w_s, jnp.zeros((), w_s.dtype))
    mixed = jnp.einsum('gts,bnsgc->bntgc', w, v) + b_s.T[None, None, :, :, None]
    y = u * mixed.reshape(B, S, GMLP_WIDTH)
    return y @ w_out


def mla_mixer(h, cos, sin, w_dqkv, q_norm_g, kv_norm_g, w_uq, w_ukv, w_o):
    B, S, _ = h.shape
    c = h @ w_dqkv
    c_q = c[..., :Q_LORA_RANK]
    c_kv = c[..., Q_LORA_RANK:Q_LORA_RANK + KV_LORA_RANK]
    k_rope = c[..., Q_LORA_RANK + KV_LORA_RANK:]
    c_q = rms_norm(c_q, q_norm_g)
    c_kv = rms_norm(c_kv, kv_norm_g)
    q = (c_q @ w_uq).reshape(B, S, MLA_HEADS, QK_NOPE_DIM + QK_ROPE_DIM)
    q_nope, q_rope = q[..., :QK_NOPE_DIM], q[..., QK_NOPE_DIM:]
    q_rope = apply_rope(q_rope, cos[:, :, None, :], sin[:, :, None, :])
    k_rope = apply_rope(k_rope, cos, sin)
    kv = (c_kv @ w_ukv).reshape(B, S, MLA_HEADS, QK_NOPE_DIM + V_HEAD_DIM)
    k_nope, v = kv[..., :QK_NOPE_DIM], kv[..., QK_NOPE_DIM:]
    scale = (QK_NOPE_DIM + QK_ROPE_DIM) ** -0.5
    nb = S // Q_BLOCK
    qn_blocks = q_nope.reshape(B, nb, Q_BLOCK, MLA_HEADS, QK_NOPE_DIM).transpose(1, 0, 2, 3, 4)
    qr_blocks = q_rope.reshape(B, nb, Q_BLOCK, MLA_HEADS, QK_ROPE_DIM).transpose(1, 0, 2, 3, 4)
    key_pos = jnp.arange(S)

    def attend(args):
        qn, qr, blk = args
        s = (jnp.einsum('bqhd,bkhd->bhqk', qn, k_nope)
             + jnp.einsum('bqhr,bkr->bhqk', qr, k_rope)).astype(jnp.float32) * scale
        q_pos = blk * Q_BLOCK + jnp.arange(Q_BLOCK)
        mask = key_pos[None, :] <= q_pos[:, None]
        s = jnp.where(mask[None, None], s, -jnp.inf)
        p = jax.nn.softmax(s, axis=-1).astype(v.dtype)
        return jnp.einsum('bhqk,bkhd->bqhd', p, v)

    o = lax.map(attend, (qn_blocks, qr_blocks, jnp.arange(nb)))
    o = o.transpose(1, 0, 2, 3, 4).reshape(B, S, MLA_HEADS * V_HEAD_DIM)
    return o @ w_o


def swiglu(h, w_gate_up, w_down):
    g, u = jnp.split(h @ w_gate_up, 2, axis=-1)
    return (jax.nn.silu(g) * u) @ w_down


def setup_inputs(seed: int = 0) -> dict:
    key = jax.random.key(seed)
    ks = jax.random.split(key, 20)
    f32 = jnp.float32

    def w(k, shape, fan_in):
        return jax.random.normal(k, shape, f32) * (fan_in ** -0.5)

    x = jax.random.normal(ks[0], (BATCH, SEQ, D_MODEL), f32)
    offset = jax.random.randint(ks[1], (BATCH, 1), 0, 4096, dtype=jnp.int32)
    positions = offset + jnp.arange(SEQ, dtype=jnp.int32)[None, :]
    norm_g = 1.0 + 0.02 * jax.random.normal(ks[2], (DEPTH, 4, D_MODEL), f32)

    gmlp_w_in = w(ks[3], (N_LAYERS_A, D_MODEL, 2 * GMLP_WIDTH), D_MODEL)
    gmlp_ln_g = 1.0 + 0.02 * jax.random.normal(ks[4], (N_LAYERS_A, GMLP_WIDTH), f32)
    gmlp_ln_b = 0.01 * jax.random.normal(ks[5], (N_LAYERS_A, GMLP_WIDTH), f32)
    gmlp_w_s = w(ks[6], (N_LAYERS_A, GMLP_GROUPS, CHUNK, CHUNK), CHUNK)
    gmlp_b_s = 1.0 + 0.01 * jax.random.normal(ks[7], (N_LAYERS_A, GMLP_GROUPS, CHUNK), f32)
    gmlp_w_out = w(ks[8], (N_LAYERS_A, GMLP_WIDTH, D_MODEL), GMLP_WIDTH)

    mla_w_dqkv = w(ks[9], (N_LAYERS_B, D_MODEL, Q_LORA_RANK + KV_LORA_RANK + QK_ROPE_DIM), D_MODEL)
    mla_q_norm_g = 1.0 + 0.02 * jax.random.normal(ks[10], (N_LAYERS_B, Q_LORA_RANK), f32)
    mla_kv_norm_g = 1.0 + 0.02 * jax.random.normal(ks[11], (N_LAYERS_B, KV_LORA_RANK), f32)
    mla_w_uq = w(ks[12], (N_LAYERS_B, Q_LORA_RANK, MLA_HEADS * (QK_NOPE_DIM + QK_ROPE_DIM)), Q_LORA_RANK)
    mla_w_ukv = w(ks[13], (N_LAYERS_B, KV_LORA_RANK, MLA_HEADS * (QK_NOPE_DIM + V_HEAD_DIM)), KV_LORA_RANK)
    mla_w_o = w(ks[14], (N_LAYERS_B, MLA_HEADS * V_HEAD_DIM, D_MODEL), MLA_HEADS * V_HEAD_DIM)

    ffn_w_gate_up = w(ks[15], (DEPTH, D_MODEL, 2 * FFN_HIDDEN), D_MODEL)
    ffn_w_down = w(ks[16], (DEPTH, FFN_HIDDEN, D_MODEL), FFN_HIDDEN)

    return {'x': x, 'positions': positions, 'norm_g': norm_g,
            'gmlp_w_in': gmlp_w_in, 'gmlp_ln_g': gmlp_ln_g, 'gmlp_ln_b': gmlp_ln_b,
            'gmlp_w_s': gmlp_w_s, 'gmlp_b_s': gmlp_b_s, 'gmlp_w_out': gmlp_w_out,
            'mla_w_dqkv': mla_w_dqkv, 'mla_q_norm_g': mla_q_norm_g, 'mla_kv_norm_g': mla_kv_norm_g,
            'mla_w_uq': mla_w_uq, 'mla_w_ukv': mla_w_ukv, 'mla_w_o': mla_w_o,
            'ffn_w_gate_up': ffn_w_gate_up, 'ffn_w_down': ffn_w_down}


def reference(x, positions, norm_g, gmlp_w_in, gmlp_ln_g, gmlp_ln_b, gmlp_w_s, gmlp_b_s,
              gmlp_w_out, mla_w_dqkv, mla_q_norm_g, mla_kv_norm_g, mla_w_uq, mla_w_ukv,
              mla_w_o, ffn_w_gate_up, ffn_w_down):
    cos, sin = rope_tables(positions)
    h = x
    for i in range(DEPTH):
        j = i // N_MIXERS
        a = rms_norm(h, norm_g[i, 0])
        if i % N_MIXERS == 0:
            m = gmlp_mixer(a, gmlp_w_in[j], gmlp_ln_g[j], gmlp_ln_b[j], gmlp_w_s[j],
                           gmlp_b_s[j], gmlp_w_out[j])
        else:
            m = mla_mixer(a, cos, sin, mla_w_dqkv[j], mla_q_norm_g[j], mla_kv_norm_g[j],
                          mla_w_uq[j], mla_w_ukv[j], mla_w_o[j])
        h = h + rms_norm(m, norm_g[i, 1])
        f = swiglu(rms_norm(h, norm_g[i, 2]), ffn_w_gate_up[i], ffn_w_down[i])
        h = h + rms_norm(f, norm_g[i, 3])
    return h
```

```python
import contextlib
import math
import types
import numpy as np
import concourse.bass as bass
import concourse.mybir as mybir
from concourse.bass_utils import run_bass_kernel_spmd

F32 = mybir.dt.float32
BF16 = mybir.dt.bfloat16
I32 = mybir.dt.int32
AF = mybir.ActivationFunctionType
ALU = mybir.AluOpType

COMPUTE = ("pe", "act", "dve", "pool")
N_DMA_SEMS = 40
T = 512
RMS_EPS = 1e-6
LN_EPS = 1e-5

CFG_FULL = dict(D=4096, S=2048, FH=11008, H=32, QL=1024, KVL=512, DEPTH=4)


class Buf:
    __slots__ = ("name", "last_write", "rd_c", "rd_d")

    def __init__(self, name=""):
        self.name = name
        self.last_write = None
        self.rd_c = {}
        self.rd_d = []


class Op:
    __slots__ = ("eng", "fn", "deps", "signal", "count", "dma_sem", "dma_val", "is_dma")

    def __init__(self, eng, fn, is_dma):
        self.eng = eng
        self.fn = fn
        self.deps = ()
        self.signal = False
        self.count = None
        self.dma_sem = None
        self.dma_val = None
        self.is_dma = is_dma


class Prog:
    def __init__(self, nc, same_engine_sync=True):
        self.nc = nc
        self.ops = {e: [] for e in COMPUTE + ("sp",)}
        self.same_engine_sync = same_engine_sync
        self.dma_rr = 0
        self.dma_last = [None] * N_DMA_SEMS
        self.last_op = {e: None for e in COMPUTE}
        self.live_dma = []
        self.pending_barrier = {}

    def barrier(self):
        deps = [op for op in self.last_op.values() if op is not None]
        deps += [op for op in self.dma_last if op is not None]
        for e in COMPUTE + ("sp",):
            self.pending_barrier[e] = list(deps)

    def add(self, eng, fn, reads=(), writes=(), is_dma=False):
        if fn.__closure__ is not None:
            cells = tuple(types.CellType(c.cell_contents) for c in fn.__closure__)
            fn = types.FunctionType(fn.__code__, fn.__globals__, fn.__name__, fn.__defaults__, cells)
        op = Op(eng, fn, is_dma)
        deps = {}
        ses = self.same_engine_sync

        def add_dep(p):
            if p is None:
                return
            if (not p.is_dma) and p.eng == eng and (eng == "pe" or not ses):
                return
            deps[id(p)] = p

        for r in reads:
            add_dep(r.last_write)
        for w in writes:
            add_dep(w.last_write)
            for rd in w.rd_c.values():
                add_dep(rd)
            for rd in w.rd_d:
                add_dep(rd)
        pb = self.pending_barrier.pop(eng, None)
        if pb:
            for p in pb:
                if p.is_dma or p.eng != eng:
                    deps[id(p)] = p
        if is_dma:
            s = self.dma_rr
            self.dma_rr = (self.dma_rr + 1) % N_DMA_SEMS
            prev = self.dma_last[s]
            if prev is not None:
                deps[id(prev)] = prev
            op.dma_sem = s
            op.dma_val = (prev.dma_val if prev is not None else 0) + 16
            self.dma_last[s] = op
        op.deps = list(deps.values())
        for p in op.deps:
            p.signal = True
        for r in reads:
            if is_dma:
                r.rd_d.append(op)
            else:
                r.rd_c[eng] = op
        for w in writes:
            w.last_write = op
            w.rd_c = {}
            w.rd_d = []
        self.ops[eng].append(op)
        if not is_dma:
            self.last_op[eng] = op
        return op

    def emit(self):
        nc = self.nc
        for e in COMPUTE:
            c = 0
            for op in self.ops[e]:
                if op.signal:
                    c += 1
                    op.count = c
        with contextlib.ExitStack() as st:
            sems = {e: st.enter_context(nc.semaphore("s_" + e)) for e in COMPUTE}
            dsems = [st.enter_context(nc.semaphore("d%d" % i)) for i in range(N_DMA_SEMS)]
            block = st.enter_context(nc.Block())
            prog = self

            def run(engname, eng):
                known = {}
                for op in prog.ops[engname]:
                    best = {}
                    for p in op.deps:
                        if p.is_dma:
                            key = ("d", p.dma_sem)
                            val = p.dma_val
                        else:
                            key = ("c", p.eng)
                            val = p.count
                        if val > best.get(key, 0):
                            best[key] = val
                    for key, val in best.items():
                        if known.get(key, 0) >= val:
                            continue
                        known[key] = val
                        sem = dsems[key[1]] if key[0] == "d" else sems[key[1]]
                        eng.wait_ge(sem, val)
                    ins = op.fn(eng)
                    if op.is_dma:
                        ins.then_inc(dsems[op.dma_sem], 16)
                    elif op.signal:
                        ins.then_inc(sems[op.eng], 1)
                if engname == "sp":
                    for s in range(N_DMA_SEMS):
                        p = prog.dma_last[s]
                        if p is not None and known.get(("d", s), 0) < p.dma_val:
                            eng.wait_ge(dsems[s], p.dma_val)

            @block.sync
            def _(e):
                run("sp", e)

            @block.tensor
            def _(e):
                run("pe", e)

            @block.scalar
            def _(e):
                run("act", e)

            @block.vector
            def _(e):
                run("dve", e)

            @block.gpsimd
            def _(e):
                run("pool", e)


class Builder:
    def __init__(self, cfg, layer_kinds):
        self.cfg = cfg
        self.kinds = layer_kinds
        D, S, FH, H, QL, KVL = (cfg[k] for k in ("D", "S", "FH", "H", "QL", "KVL"))
        self.D, self.S, self.FH, self.H, self.QL, self.KVL = D, S, FH, H, QL, KVL
        self.DC = D // 128
        self.NT = S // T
        nc = bass.Bass("TRN2", target_bir_lowering=False)
        self.nc = nc
        self.P = Prog(nc)
        dt = nc.dram_tensor
        self.xT = dt("xT", [D, S], F32, kind="ExternalInput").ap()
        self.outT = dt("outT", [D, S], F32, kind="ExternalOutput").ap()
        self.cst_d = dt("cst", [128, 512], F32, kind="ExternalInput").ap()
        NL = len(layer_kinds)
        self.NV = NL * 4 * self.DC + NL * (2 * self.DC + QL // 128 + KVL // 128)
        self.vec_d = dt("vec", [128, self.NV], F32, kind="ExternalInput").ap()
        self.w = []
        need_pos = False
        for li, k in enumerate(layer_kinds):
            d = {}
            if k in ("g", "G"):
                d["win"] = dt("win%d" % li, [D, 2 * D], F32, kind="ExternalInput").ap()
                d["wout"] = dt("wout%d" % li, [D, D], F32, kind="ExternalInput").ap()
                d["ws"] = dt("ws%d" % li, [self.DC, 128, 128], F32, kind="ExternalInput").ap()
                d["bs"] = dt("bs%d" % li, [128, self.DC * 128], F32, kind="ExternalInput").ap()
            elif k in ("m", "M"):
                need_pos = True
                d["wdqkv"] = dt("wdqkv%d" % li, [D, QL + KVL + 64], F32, kind="ExternalInput").ap()
                d["wuq"] = dt("wuq%d" % li, [QL, H * 192], F32, kind="ExternalInput").ap()
                d["wukv"] = dt("wukv%d" % li, [KVL, H * 256], F32, kind="ExternalInput").ap()
                d["wo"] = dt("wo%d" % li, [H * 128, D], F32, kind="ExternalInput").ap()
            d["wgu"] = dt("wgu%d" % li, [D, 2 * FH], F32, kind="ExternalInput").ap()
            d["wdn"] = dt("wdn%d" % li, [FH, D], F32, kind="ExternalInput").ap()
            self.w.append(d)
        self.need_pos = need_pos
        self.hA = dt("hscrA", [D, S], F32, kind="Internal").ap()
        import os
        self.hB = dt("hscrB", [D, S], F32, kind="ExternalOutput" if os.environ.get("KDEBUG") else "Internal").ap()
        if need_pos:
            self.pos_d = dt("pos", [64, S], I32, kind="ExternalInput").ap()
            self.oscr = dt("oscr", [H * 128, S], BF16, kind="Internal").ap()
        self.hb = {}
        self.cast_rr = 0
        self.ws_i = 0

    def hbuf(self, which, t, cg):
        key = (which, t, cg)
        b = self.hb.get(key)
        if b is None:
            b = Buf("h%s_%d_%d" % key)
            self.hb[key] = b
        return b

    def sb(self, st, name, shape, dtype):
        self.sb_i = getattr(self, "sb_i", 0) + 1
        return st.enter_context(self.nc.sbuf_tensor("sb%d_%s" % (self.sb_i, name), shape, dtype))

    def cast_engine(self):
        seq = ("dve", "act", "dve", "act", "dve", "act", "dve", "act", "pool")
        e = seq[self.cast_rr % len(seq)]
        self.cast_rr += 1
        return e

    def emit_copy(self, eng, out, in_, reads, writes):
        P = self.P
        if eng == "act":
            return P.add("act", lambda e: e.activation(out=out, in_=in_, func=AF.Copy), reads=reads, writes=writes)
        return P.add(eng, lambda e: e.tensor_copy(out=out, in_=in_), reads=reads, writes=writes)

    def wload(self, pieces, a, b):
        P = self.P
        k = self.ws_i % len(self.wbf)
        self.ws_i += 1
        bfv = self.wbf[k][:, 0:a * b].rearrange("p (a b) -> p a b", a=a)
        nh = 2 if (a % 2 == 0 and a * b > 2048) else 1
        ah = a // nh
        bl = []
        for hh in range(nh):
            sidx = self.st_i % len(self.wstage)
            self.st_i += 1
            stg = self.wstage[sidx][:, 0:ah * b].rearrange("p (a b) -> p a b", a=ah)
            sbufs = self.wstage_b[sidx]
            for pi, (lo, hi, ap) in enumerate(pieces):
                aph = ap[:, hh * ah:(hh + 1) * ah, :]
                P.add("sp", lambda e, lo=lo, hi=hi, aph=aph, stg=stg: e.dma_start(out=stg[:, :, lo:hi], in_=aph),
                      writes=[sbufs[pi]], is_dma=True)
            eng = self.cast_engine()
            hb = self.wbf_b[k][hh]
            wr = [hb] if (nh == 1 or a * b == 4096) else list(self.wbf_b[k])
            self.emit_copy(eng, bfv[:, hh * ah:(hh + 1) * ah, :], stg, reads=sbufs[:len(pieces)], writes=wr)
            bl += [hb] * ah
        return bfv, bl

    def wload_rows(self, wap, r0, nrow_chunks, c0, ncols):
        ap = wap[r0:r0 + nrow_chunks * 128, c0:c0 + ncols].rearrange("(j p) c -> p j c", p=128)
        return self.wload([(0, ncols, ap)], nrow_chunks, ncols)

    def alloc_wpool(self, st, nbf):
        self.wstage = [self.sb(st, "wst%d" % i, [128, 2048], F32) for i in range(4)]
        self.wstage_b = [[Buf() for _ in range(4)] for i in range(4)]
        self.wbf = [self.sb(st, "wbf%d" % i, [128, 4096], BF16) for i in range(nbf)]
        self.wbf_b = [[Buf(), Buf()] for _ in range(nbf)]

    def pipeline(self, steps):
        nxt = steps[0][0]() if steps else None
        for i, (ld, cp) in enumerate(steps):
            cur = nxt
            nxt = steps[i + 1][0]() if i + 1 < len(steps) else None
            cp(cur)

    def rsqrt_inplace(self, ap, b):
        P = self.P
        P.add("act", lambda e: e.activation(out=ap, in_=ap, func=AF.Sqrt), reads=[b], writes=[b])
        P.add("dve", lambda e: e.reciprocal(out=ap, in_=ap), reads=[b], writes=[b])

    def sumsq_rstd(self, src, src_bufs, nchunks, sq_tmp, sq_bufs, rstd, rstd_b, dim, eps):
        P = self.P
        ps, psb = self.ps[6], self.ps_b[6]
        for c in range(nchunks):
            eng = "act" if c % 2 == 0 else "pool"
            if eng == "act":
                P.add("act", lambda e, c=c: e.activation(out=sq_tmp[:, c, :], in_=src[:, c, :], func=AF.Square),
                      reads=[src_bufs[c]], writes=[sq_bufs[c]])
            else:
                P.add("pool", lambda e, c=c: e.tensor_tensor(out=sq_tmp[:, c, :], in0=src[:, c, :], in1=src[:, c, :], op=ALU.mult),
                      reads=[src_bufs[c]], writes=[sq_bufs[c]])
        for c in range(nchunks):
            P.add("pe", lambda e, c=c: e.matmul(ps[:], lhsT=self.ones_bf[:], rhs=sq_tmp[:, c, :], start=(c == 0), stop=(c == nchunks - 1)),
                  reads=[sq_bufs[c], self.const_b], writes=[psb])
        P.add("dve", lambda e: e.tensor_scalar(out=rstd[:], in0=ps[:], scalar1=1.0 / dim, scalar2=eps, op0=ALU.mult, op1=ALU.add),
              reads=[psb], writes=[rstd_b])
        self.rsqrt_inplace(rstd[:], rstd_b)

    def load_norm(self, st_unused, src_ap, which, t, gcol0):
        P = self.P
        DC = self.DC
        abuf = self.abuf
        cpg = min(2, DC)
        ncg = DC // cpg
        tok = slice(t * T, (t + 1) * T)
        for ps_ in range(2):
            for cg in range(ncg):
                slot = self.tmp_i % 2
                self.tmp_i += 1
                tmp, tb = self.tmp[slot], self.tmp_b[slot]
                ap = src_ap[cg * cpg * 128:(cg + 1) * cpg * 128, tok].rearrange("(c p) t -> p c t", p=128)
                P.add("sp", lambda e, tmp=tmp, ap=ap: e.dma_start(out=tmp[:, 0:cpg, :], in_=ap),
                      reads=[self.hbuf(which, t, cg)], writes=[tb], is_dma=True)
                if ps_ == 0:
                    if cg == 0:
                        pass
                    for c in range(cpg):
                        cc = cg * cpg + c
                        eng = "act" if cc % 2 == 0 else "pool"
                        if eng == "act":
                            P.add("act", lambda e, tmp=tmp, c=c, cc=cc: e.activation(out=abuf[:, cc, :], in_=tmp[:, c, :], func=AF.Square),
                                  reads=[tb], writes=[self.abuf_b[cc]])
                        else:
                            P.add("pool", lambda e, tmp=tmp, c=c, cc=cc: e.tensor_tensor(out=abuf[:, cc, :], in0=tmp[:, c, :], in1=tmp[:, c, :], op=ALU.mult),
                                  reads=[tb], writes=[self.abuf_b[cc]])
                else:
                    for c in range(cpg):
                        cc = cg * cpg + c
                        P.add("dve", lambda e, tmp=tmp, c=c, cc=cc: e.scalar_tensor_tensor(
                            out=abuf[:, cc, :], in0=tmp[:, c, :], scalar=self.vec[:, gcol0 + cc:gcol0 + cc + 1],
                            in1=self.rstd[:], op0=ALU.mult, op1=ALU.mult),
                            reads=[tb, self.rstd_b, self.const_b], writes=[self.abuf_b[cc]])
            if ps_ == 0:
                ps, psb = self.ps[6], self.ps_b[6]
                for cc in range(DC):
                    P.add("pe", lambda e, cc=cc: e.matmul(ps[:], lhsT=self.ones_bf[:], rhs=abuf[:, cc, :], start=(cc == 0), stop=(cc == DC - 1)),
                          reads=[self.abuf_b[cc], self.const_b], writes=[psb])
                P.add("dve", lambda e: e.tensor_scalar(out=self.rstd[:], in0=ps[:], scalar1=1.0 / self.D, scalar2=RMS_EPS, op0=ALU.mult, op1=ALU.add),
                      reads=[psb], writes=[self.rstd_b])
                self.rsqrt_inplace(self.rstd[:], self.rstd_b)

    def post_norm_residual(self, src_ap, which, t, gcol0, sq_tmp, sq_bufs, dst_ap=None, dwhich=None):
        P = self.P
        DC = self.DC
        macc = self.macc
        tok = slice(t * T, (t + 1) * T)
        self.sumsq_rstd(self.macc, self.macc_b, DC, sq_tmp, sq_bufs, self.rstd, self.rstd_b, self.D, RMS_EPS)
        cpg = min(2, DC)
        ncg = DC // cpg
        for cg in range(ncg):
            slot = self.tmp_i % 2
            self.tmp_i += 1
            tmp, tb = self.tmp[slot], self.tmp_b[slot]
            ap = src_ap[cg * cpg * 128:(cg + 1) * cpg * 128, tok].rearrange("(c p) t -> p c t", p=128)
            P.add("sp", lambda e, tmp=tmp, ap=ap: e.dma_start(out=tmp[:, 0:cpg, :], in_=ap),
                  reads=[self.hbuf(which, t, cg)], writes=[tb], is_dma=True)
            for c in range(cpg):
                cc = cg * cpg + c
                P.add("dve", lambda e, cc=cc: e.scalar_tensor_tensor(
                    out=macc[:, cc, :], in0=macc[:, cc, :], scalar=self.vec[:, gcol0 + cc:gcol0 + cc + 1],
                    in1=self.rstd[:], op0=ALU.mult, op1=ALU.mult),
                    reads=[self.macc_b[cc], self.rstd_b, self.const_b], writes=[self.macc_b[cc]])
            rb = [self.macc_b[cg * cpg + c] for c in range(cpg)]
            P.add("pool", lambda e, tmp=tmp, cg=cg: e.tensor_tensor(out=tmp[:, 0:cpg, :], in0=tmp[:, 0:cpg, :],
                                                                     in1=macc[:, cg * cpg:(cg + 1) * cpg, :], op=ALU.add),
                  reads=rb + [tb], writes=[tb])
            oap = self.dst_ap[cg * cpg * 128:(cg + 1) * cpg * 128, tok].rearrange("(c p) t -> p c t", p=128)
            P.add("sp", lambda e, tmp=tmp, oap=oap: e.dma_start(out=oap, in_=tmp[:, 0:cpg, :]),
                  reads=[tb], writes=[self.hbuf(self.dwhich, t, cg)], is_dma=True)

    def proj_to_macc(self, wap, rhs, rhs_bufs, KC, banks=(4, 5)):
        P = self.P
        macc = self.macc
        macc_b = self.macc_b
        steps = []
        for oc in range(self.DC):
            def ld(oc=oc):
                return self.wload_rows(wap, 0, KC, oc * 128, 128)

            def cp(hnd, oc=oc):
                wt, wb = hnd
                bk = banks[oc % 2]
                ps, psb = self.ps[bk], self.ps_b[bk]
                for kc in range(KC):
                    P.add("pe", lambda e, wt=wt, kc=kc, ps=ps: e.matmul(ps[:], lhsT=wt[:, kc, :], rhs=rhs[:, kc, :], start=(kc == 0), stop=(kc == KC - 1)),
                          reads=[wb[kc], rhs_bufs[kc]], writes=[psb])
                eng = "act" if oc % 2 == 0 else "dve"
                self.emit_copy(eng, macc[:, oc, :], ps[:], reads=[psb], writes=[macc_b[oc]])
            steps.append((ld, cp))
        self.pipeline(steps)

    def ffn(self, li, src_ap, which):
        P = self.P
        nc = self.nc
        D, FH, DC = self.D, self.FH, self.DC
        FC = FH // 128
        G = 8
        wgu, wdn = self.w[li]["wgu"], self.w[li]["wdn"]
        g2 = li * 4 * DC + 2 * DC
        g3 = li * 4 * DC + 3 * DC
        with contextlib.ExitStack() as st:
            self.alloc_wpool(st, 4)
            self.abuf = self.sb(st, "f_abuf", [128, DC, T], BF16)
            self.abuf_b = [Buf() for _ in range(DC)]
            self.macc = self.sb(st, "f_macc", [128, DC, T], F32)
            self.macc_b = [Buf() for _ in range(DC)]
            abuf, macc = self.abuf, self.macc
            hid = [self.sb(st, "f_hid%d" % i, [128, G, T], BF16) for i in range(2)]
            hid_b = [[Buf() for _ in range(G)] for i in range(2)]
            sg = [self.sb(st, "f_sg%d" % i, [128, T], F32) for i in range(2)]
            sg_b = [Buf(), Buf()]
            abuf_b = self.abuf_b
            macc_b = self.macc_b
            ngroups = (FC + G - 1) // G
            ncg = max(1, D // 512)
            cw = min(512, D)
            for t in range(self.NT):
                self.load_norm(st, src_ap, which, t, g2)
                steps = []
                fci = 0
                for gi in range(ngroups):
                    f0 = gi * G
                    gsz = min(G, FC - f0)
                    hs, hsb = hid[gi % 2], hid_b[gi % 2]
                    for j in range(gsz):
                        fc = f0 + j

                        def ld(fc=fc):
                            return (self.wload_rows(wgu, 0, DC, fc * 128, 128), self.wload_rows(wgu, 0, DC, FH + fc * 128, 128))

                        def cp(hnd, j=j, fci=fci, hs=hs, hsb=hsb):
                            (wg_t, wg_b), (wu_t, wu_b) = hnd
                            pg, pgb = self.ps[fci % 2], self.ps_b[fci % 2]
                            pu, pub = self.ps[2 + fci % 2], self.ps_b[2 + fci % 2]
                            for kc in range(DC):
                                P.add("pe", lambda e, wt=wg_t, kc=kc, ps=pg: e.matmul(ps[:], lhsT=wt[:, kc, :], rhs=abuf[:, kc, :], start=(kc == 0), stop=(kc == DC - 1)),
                                      reads=[wg_b[kc], abuf_b[kc]], writes=[pgb])
                            for kc in range(DC):
                                P.add("pe", lambda e, wt=wu_t, kc=kc, ps=pu: e.matmul(ps[:], lhsT=wt[:, kc, :], rhs=abuf[:, kc, :], start=(kc == 0), stop=(kc == DC - 1)),
                                      reads=[wu_b[kc], abuf_b[kc]], writes=[pub])
                            sgt, sgb = sg[fci % 2], sg_b[fci % 2]
                            P.add("act", lambda e, sgt=sgt, pg=pg: e.activation(out=sgt[:], in_=pg[:], func=AF.Silu),
                                  reads=[pgb], writes=[sgb])
                            P.add("dve", lambda e, hs=hs, j=j, sgt=sgt, pu=pu: e.tensor_tensor(out=hs[:, j, :], in0=sgt[:], in1=pu[:], op=ALU.mult),
                                  reads=[sgb, pub], writes=[hsb[j]])
                        steps.append((ld, cp))
                        fci += 1
                    for dcg in range(ncg):
                        def ld(f0=f0, gsz=gsz, dcg=dcg):
                            ap = wdn[f0 * 128:(f0 + gsz) * 128, dcg * cw:(dcg + 1) * cw].rearrange("(j p) c -> p j c", p=128)
                            return self.wload([(0, cw, ap)], gsz, cw)

                        def cp(hnd, gi=gi, gsz=gsz, dcg=dcg, hs=hs, hsb=hsb):
                            wd_t, wd_b = hnd
                            for dc in range(cw // 128):
                                oc = dcg * (cw // 128) + dc
                                bk = 4 + oc % 2
                                ps, psb = self.ps[bk], self.ps_b[bk]
                                for j in range(gsz):
                                    P.add("pe", lambda e, wt=wd_t, j=j, dc=dc, ps=ps, hs=hs: e.matmul(
                                        ps[:], lhsT=wt[:, j, dc * 128:(dc + 1) * 128], rhs=hs[:, j, :], start=(j == 0), stop=(j == gsz - 1)),
                                        reads=[wd_b[j], hsb[j]], writes=[psb])
                                if gi == 0:
                                    self.emit_copy("act", macc[:, oc, :], ps[:], reads=[psb], writes=[macc_b[oc]])
                                else:
                                    P.add("dve", lambda e, oc=oc, ps=ps: e.tensor_tensor(out=macc[:, oc, :], in0=macc[:, oc, :], in1=ps[:], op=ALU.add),
                                          reads=[psb, macc_b[oc]], writes=[macc_b[oc]])
                        steps.append((ld, cp))
                self.pipeline(steps)
                self.post_norm_residual(src_ap, which, t, g3, self.abuf, self.abuf_b)
            P.barrier()

    def gmlp(self, li, src_ap, which):
        P = self.P
        nc = self.nc
        D, DC = self.D, self.DC
        w = self.w[li]
        g0 = li * 4 * DC
        g1 = g0 + DC
        NLf = len(self.kinds)
        xb = NLf * 4 * DC + li * (2 * DC + self.QL // 128 + self.KVL // 128)
        lng0 = xb
        lnb0 = xb + DC
        NB = T // 128
        VG = max(1, D // 512)
        vw = min(512, D)
        with contextlib.ExitStack() as st:
            self.alloc_wpool(st, 3)
            yT = self.sb(st, "g_yT", [128, DC, T], BF16)
            yT_b = [Buf() for _ in range(DC)]
            bias2 = self.sb(st, "g_bias2", [128, DC, 128], F32)
            bias2_b = Buf()
            wmT = self.sb(st, "g_wmT", [128, DC, 128], BF16)
            wmT_b = [Buf() for _ in range(DC)]
            with contextlib.ExitStack() as st2:
                wsf = self.sb(st2, "g_wsf", [128, DC, 128], F32)
                wsf_b = Buf()
                bsb = self.sb(st2, "g_bsb", [128, DC, 128], F32)
                bsb_b = Buf()
                P.add("sp", lambda e: e.dma_start(out=wsf[:], in_=w["ws"].rearrange("g t s -> t g s")), writes=[wsf_b], is_dma=True)
                P.add("sp", lambda e: e.dma_start(out=bsb[:], in_=w["bs"].rearrange("p (g t) -> p g t", g=DC)), writes=[bsb_b], is_dma=True)
                for g in range(DC):
                    bk = g % 2
                    ps, psb = self.ps[bk], self.ps_b[bk]
                    P.add("pe", lambda e, g=g, ps=ps: e.transpose(out=ps[:, 0:128], in_=wsf[:, g, :], identity=self.ident[:]),
                          reads=[wsf_b, self.const_b], writes=[psb])
                    P.add("dve", lambda e, g=g, ps=ps: e.tensor_tensor(out=wmT[:, g, :], in0=ps[:, 0:128], in1=self.tri_f[:], op=ALU.mult),
                          reads=[psb, self.const_b], writes=[wmT_b[g]])
                for g in range(DC):
                    bk = 2 + g % 2
                    ps, psb = self.ps[bk], self.ps_b[bk]
                    P.add("pe", lambda e, g=g, ps=ps: e.matmul(ps[:, 0:128], lhsT=self.ones_bf[:], rhs=wmT[:, g, :], start=True, stop=True),
                          reads=[wmT_b[g], self.const_b], writes=[psb])
                    P.add("dve", lambda e, g=g, ps=ps: e.scalar_tensor_tensor(
                        out=bias2[:, g, :], in0=ps[:, 0:128], scalar=self.vec[:, lnb0 + g:lnb0 + g + 1], in1=bsb[:, g, :],
                        op0=ALU.mult, op1=ALU.add), reads=[psb, bsb_b, self.const_b], writes=[bias2_b])
                P.barrier()
            for t in range(self.NT):
                with contextlib.ExitStack() as st2:
                    self.abuf = self.sb(st2, "g_abuf", [128, DC, T], BF16)
                    self.abuf_b = [Buf() for _ in range(DC)]
                    abuf = self.abuf
                    vb = self.sb(st2, "g_v", [128, NB, D], BF16)
                    vb_b = [Buf() for _ in range(NB)]
                    vf = [self.sb(st2, "g_vf%d" % i, [128, 512], F32) for i in range(2)]
                    vf_b = [Buf(), Buf()]
                    SD = nc.vector.BN_STATS_DIM
                    stats = self.sb(st2, "g_stats", [128, NB, VG * SD], F32)
                    stats_b = [Buf() for _ in range(NB)]
                    mv = self.sb(st2, "g_mv", [128, NB, 2], F32)
                    mv_b = [Buf() for _ in range(NB)]
                    uf = [self.sb(st2, "g_uf%d" % i, [128, T], F32) for i in range(2)]
                    uf_b = [Buf(), Buf()]
                    mt = [self.sb(st2, "g_mt%d" % i, [128, T], F32) for i in range(2)]
                    mt_b = [Buf(), Buf()]
                    self.load_norm(st2, src_ap, which, t, g0)
                    abuf_b = self.abuf_b
                    KCs = min(DC, 4096 // vw)
                    nks = DC // KCs
                    steps = []
                    for vg in range(VG):
                        for ks in range(nks):
                            def ld(vg=vg, ks=ks):
                                return self.wload_rows(w["win"], ks * KCs * 128, KCs, D + vg * vw, vw)

                            def cp(hnd, vg=vg, ks=ks):
                                wt, wtb = hnd
                                for tb in range(NB):
                                    ps, psb = self.ps[tb], self.ps_b[tb]
                                    for k2 in range(KCs):
                                        kc = ks * KCs + k2
                                        P.add("pe", lambda e, wt=wt, kc=kc, k2=k2, tb=tb, ps=ps: e.matmul(
                                            ps[:, 0:vw], lhsT=abuf[:, kc, tb * 128:(tb + 1) * 128], rhs=wt[:, k2, :],
                                            start=(kc == 0), stop=(kc == DC - 1)), reads=[wtb[k2], abuf_b[kc]], writes=[psb])
                                if ks == nks - 1:
                                    for tb in range(NB):
                                        ps, psb = self.ps[tb], self.ps_b[tb]
                                        vi = vg * NB + tb
                                        vft, vfb = vf[vi % 2], vf_b[vi % 2]
                                        P.add("act", lambda e, vft=vft, ps=ps: e.activation(out=vft[:, 0:vw], in_=ps[:, 0:vw], func=AF.Gelu),
                                              reads=[psb], writes=[vfb])
                                        P.add("dve", lambda e, vft=vft, tb=tb, vg=vg: e.bn_stats(out=stats[:, tb, vg * SD:(vg + 1) * SD], in_=vft[:, 0:vw]),
                                              reads=[vfb], writes=[stats_b[tb]])
                                        P.add("pool", lambda e, vft=vft, tb=tb, vg=vg: e.tensor_copy(out=vb[:, tb, vg * vw:(vg + 1) * vw], in_=vft[:, 0:vw]),
                                              reads=[vfb], writes=[vb_b[tb]])
                            steps.append((ld, cp))
                    self.pipeline(steps)
                    for tb in range(NB):
                        P.add("dve", lambda e, tb=tb: e.bn_aggr(out=mv[:, tb, :], in_=stats[:, tb, :]),
                              reads=[stats_b[tb]], writes=[mv_b[tb]])
                        P.add("dve", lambda e, tb=tb: e.tensor_scalar_add(out=mv[:, tb, 1:2], in0=mv[:, tb, 1:2], scalar1=LN_EPS),
                              reads=[mv_b[tb]], writes=[mv_b[tb]])
                        self.rsqrt_inplace(mv[:, tb, 1:2], mv_b[tb])
                        P.add("dve", lambda e, tb=tb: e.tensor_scalar(out=vb[:, tb, :], in0=vb[:, tb, :], scalar1=mv[:, tb, 0:1], scalar2=mv[:, tb, 1:2],
                                                                      op0=ALU.subtract, op1=ALU.mult),
                              reads=[mv_b[tb], vb_b[tb]], writes=[vb_b[tb]])
                    steps = []
                    for g in range(DC):
                        def ld(g=g):
                            return self.wload_rows(w["win"], 0, DC, g * 128, 128)

                        def cp(hnd, g=g):
                            wt, wtb = hnd
                            pu, pub = self.ps[4 + g % 2], self.ps_b[4 + g % 2]
                            for kc in range(DC):
                                P.add("pe", lambda e, wt=wt, kc=kc, pu=pu: e.matmul(pu[:], lhsT=wt[:, kc, :], rhs=abuf[:, kc, :], start=(kc == 0), stop=(kc == DC - 1)),
                                      reads=[wtb[kc], abuf_b[kc]], writes=[pub])
                            uft, ufb = uf[g % 2], uf_b[g % 2]
                            P.add("act", lambda e, uft=uft, pu=pu: e.activation(out=uft[:], in_=pu[:], func=AF.Gelu), reads=[pub], writes=[ufb])
                            pm, pmb = self.ps[6 + g % 2], self.ps_b[6 + g % 2]
                            for tb in range(NB):
                                P.add("pe", lambda e, g=g, tb=tb, pm=pm: e.matmul(pm[:, tb * 128:(tb + 1) * 128], lhsT=vb[:, tb, g * 128:(g + 1) * 128],
                                                                              rhs=wmT[:, g, :], start=True, stop=True),
                                      reads=[vb_b[tb], wmT_b[g]], writes=[pmb])
                            mtt, mtb = mt[g % 2], mt_b[g % 2]
                            P.add("dve", lambda e, g=g, pm=pm, mtt=mtt: e.scalar_tensor_tensor(
                                out=mtt[:].rearrange("p (b t) -> p b t", b=NB), in0=pm[:].rearrange("p (b t) -> p b t", b=NB),
                                scalar=self.vec[:, lng0 + g:lng0 + g + 1],
                                in1=bias2[:, g:g + 1, :].to_broadcast([128, NB, 128]),
                                op0=ALU.mult, op1=ALU.add), reads=[pmb, bias2_b, self.const_b], writes=[mtb])
                            P.add("pool", lambda e, g=g, mtt=mtt, uft=uft: e.tensor_tensor(out=yT[:, g, :], in0=mtt[:], in1=uft[:], op=ALU.mult),
                                  reads=[mtb, ufb], writes=[yT_b[g]])
                        steps.append((ld, cp))
                    self.pipeline(steps)
                    P.barrier()
                with contextlib.ExitStack() as st2:
                    self.macc = self.sb(st2, "g_macc", [128, DC, T], F32)
                    self.macc_b = [Buf() for _ in range(DC)]
                    self.proj_to_macc(w["wout"], yT, yT_b, DC)
                    self.post_norm_residual(src_ap, which, t, g1, yT, yT_b)
                    P.barrier()
            P.barrier()


    def mla(self, li, src_ap, which):
        P = self.P
        nc = self.nc
        D, DC, S, H, QL, KVL = self.D, self.DC, self.S, self.H, self.QL, self.KVL
        NT = self.NT
        QC, KVC = QL // 128, KVL // 128
        CC = QC + KVC
        w = self.w[li]
        g0 = li * 4 * DC
        g1 = g0 + DC
        NLf = len(self.kinds)
        xb = NLf * 4 * DC + li * (2 * DC + QC + KVC)
        qg0 = xb + 2 * DC
        kvg0 = qg0 + QC
        scale = float((128 + 64) ** -0.5)
        NTB = S // 128
        PI = math.pi
        oscr_b = [[Buf() for _ in range(NT)] for _ in range(H)]
        with contextlib.ExitStack() as st:
            self.alloc_wpool(st, 3)
            cqn = self.sb(st, "m_cqn", [128, QC, S], BF16)
            cqn_b = [Buf() for _ in range(NT)]
            ckvn = self.sb(st, "m_ckvn", [128, KVC, S], BF16)
            ckvn_b = [Buf() for _ in range(NT)]
            krT = self.sb(st, "m_krT", [64, S], BF16)
            krT_b = [Buf() for _ in range(NT)]
            cos2 = self.sb(st, "m_cos2", [64, S], F32)
            sinS = self.sb(st, "m_sinS", [64, S], F32)
            rope_b = Buf()
            with contextlib.ExitStack() as st2:
                posi = self.sb(st2, "m_posi", [64, S], I32)
                posf = self.sb(st2, "m_posf", [64, S], F32)
                pb_ = Buf()
                P.add("sp", lambda e: e.dma_start(out=posi[:], in_=self.pos_d), writes=[pb_], is_dma=True)
                P.add("dve", lambda e: e.tensor_copy(out=posf[:], in_=posi[:]), reads=[pb_], writes=[pb_])
                yq = self.sb(st2, "m_yq", [64, S], F32)
                ki = self.sb(st2, "m_ki", [64, S], I32)
                for tab, pc in ((cos2, 257), (sinS, 258)):
                    P.add("dve", lambda e, tab=tab, pc=pc: e.tensor_scalar(out=yq[:], in0=posf[:], scalar1=self.cst[0:64, 256:257],
                                                                       scalar2=self.cst[0:64, pc:pc + 1], op0=ALU.mult, op1=ALU.add),
                          reads=[pb_, self.const_b], writes=[rope_b])
                    P.add("dve", lambda e: e.tensor_copy(out=ki[:], in_=yq[:]), reads=[rope_b], writes=[rope_b])
                    P.add("dve", lambda e, tab=tab: e.tensor_copy(out=tab[:], in_=ki[:]), reads=[rope_b], writes=[rope_b])
                    P.add("dve", lambda e, tab=tab: e.tensor_tensor(out=yq[:], in0=yq[:], in1=tab[:], op=ALU.subtract), reads=[rope_b], writes=[rope_b])
                    P.add("dve", lambda e, tab=tab: e.tensor_single_scalar(out=tab[:], in_=yq[:], scalar=0.5, op=ALU.is_gt), reads=[rope_b], writes=[rope_b])
                    P.add("dve", lambda e, tab=tab: e.tensor_tensor(out=yq[:], in0=yq[:], in1=tab[:], op=ALU.subtract), reads=[rope_b], writes=[rope_b])
                    P.add("dve", lambda e, tab=tab: e.tensor_single_scalar(out=tab[:], in_=yq[:], scalar=-0.5, op=ALU.is_lt), reads=[rope_b], writes=[rope_b])
                    P.add("dve", lambda e, tab=tab: e.tensor_tensor(out=yq[:], in0=yq[:], in1=tab[:], op=ALU.add), reads=[rope_b], writes=[rope_b])
                    P.add("act", lambda e, tab=tab: e.activation(out=tab[:], in_=yq[:], func=AF.Sin, scale=2 * PI * (1.0 - 1e-6)),
                          reads=[rope_b], writes=[rope_b])
                P.barrier()
            with contextlib.ExitStack() as st2:
                self.abuf = self.sb(st2, "m_abuf", [128, DC, T], BF16)
                self.abuf_b = [Buf() for _ in range(DC)]
                abuf = self.abuf
                cf = self.sb(st2, "m_cf", [128, CC, T], F32)
                cf_b = [Buf() for _ in range(CC)]
                krf = self.sb(st2, "m_krf", [64, T], F32)
                krs = self.sb(st2, "m_krs", [64, T], F32)
                krf_b, krs_b = Buf(), Buf()
                for t in range(NT):
                    tok = slice(t * T, (t + 1) * T)
                    self.load_norm(st2, src_ap, which, t, g0)
                    abuf_b = self.abuf_b
                    steps = []
                    for oc in range(CC):
                        def ld(oc=oc):
                            return self.wload_rows(w["wdqkv"], 0, DC, oc * 128, 128)

                        def cp(hnd, oc=oc):
                            wt, wtb = hnd
                            ps, psb = self.ps[oc % 2], self.ps_b[oc % 2]
                            for kc in range(DC):
                                P.add("pe", lambda e, wt=wt, kc=kc, ps=ps: e.matmul(ps[:], lhsT=wt[:, kc, :], rhs=abuf[:, kc, :], start=(kc == 0), stop=(kc == DC - 1)),
                                      reads=[wtb[kc], abuf_b[kc]], writes=[psb])
                            self.emit_copy("act" if oc % 2 == 0 else "dve", cf[:, oc, :], ps[:], reads=[psb], writes=[cf_b[oc]])
                        steps.append((ld, cp))
                    self.pipeline(steps)
                    c0 = QL + KVL
                    wd = w["wdqkv"]
                    def colap(lo, hi):
                        return wd[:, lo:hi].rearrange("(j p) c -> p j c", p=128)
                    wt, wtb = self.wload([(0, 64, colap(c0, c0 + 64)), (64, 96, colap(c0 + 32, c0 + 64)), (96, 128, colap(c0, c0 + 32))], DC, 128)
                    p2, p2b = self.ps[2], self.ps_b[2]
                    p3, p3b = self.ps[3], self.ps_b[3]
                    for kc in range(DC):
                        P.add("pe", lambda e, wt=wt, kc=kc: e.matmul(p2[0:64, :], lhsT=wt[:, kc, 0:64], rhs=abuf[:, kc, :], start=(kc == 0), stop=(kc == DC - 1)),
                              reads=[wtb[kc], self.abuf_b[kc]], writes=[p2b])
                    for kc in range(DC):
                        P.add("pe", lambda e, wt=wt, kc=kc: e.matmul(p3[0:64, :], lhsT=wt[:, kc, 64:128], rhs=abuf[:, kc, :], start=(kc == 0), stop=(kc == DC - 1)),
                              reads=[wtb[kc], self.abuf_b[kc]], writes=[p3b])
                    P.add("dve", lambda e, tok=tok: e.tensor_tensor(out=krf[:], in0=p2[0:64, :], in1=cos2[:, tok], op=ALU.mult),
                          reads=[p2b, rope_b], writes=[krf_b])
                    P.add("dve", lambda e, tok=tok: e.tensor_tensor(out=krs[:], in0=p3[0:64, :], in1=sinS[:, tok], op=ALU.mult),
                          reads=[p3b, rope_b], writes=[krs_b])
                    P.add("pool", lambda e, tok=tok: e.tensor_tensor(out=krT[:, tok], in0=krf[:], in1=krs[:], op=ALU.add),
                          reads=[krf_b, krs_b], writes=[krT_b[t]])
                    for (lo, n, dim, gc0, dst, dst_b) in ((0, QC, QL, qg0, cqn, cqn_b), (QC, KVC, KVL, kvg0, ckvn, ckvn_b)):
                        self.sumsq_rstd(cf[:, lo:lo + n, :], cf_b[lo:lo + n], n, abuf[:, lo:lo + n, :], self.abuf_b[lo:lo + n],
                                        self.rstd, self.rstd_b, dim, RMS_EPS)
                        for c in range(n):
                            P.add("dve", lambda e, c=c, lo=lo, gc0=gc0, dst=dst, tok=tok: e.scalar_tensor_tensor(
                                out=dst[:, c, tok], in0=cf[:, lo + c, :], scalar=self.vec[:, gc0 + c:gc0 + c + 1], in1=self.rstd[:],
                                op0=ALU.mult, op1=ALU.mult), reads=[cf_b[lo + c], self.rstd_b, self.const_b], writes=[dst_b[t]])
                P.barrier()
            with contextlib.ExitStack() as st2:
                qn = [self.sb(st2, "m_qn%d" % i, [128, S], BF16) for i in range(2)]
                qr = [self.sb(st2, "m_qr%d" % i, [64, S], BF16) for i in range(2)]
                kn = [self.sb(st2, "m_kn%d" % i, [128, S], BF16) for i in range(2)]
                vv = [self.sb(st2, "m_vv%d" % i, [128, NTB, 128], BF16) for i in range(2)]
                qn_b = [[Buf() for _ in range(NT)] for i in range(2)]
                qr_b = [[Buf() for _ in range(NT)] for i in range(2)]
                kn_b = [[Buf() for _ in range(NT)] for i in range(2)]
                vv_b = [[Buf() for _ in range(NT)] for i in range(2)]
                rf1 = self.sb(st2, "m_rf1", [64, T], F32)
                rf2 = self.sb(st2, "m_rf2", [64, T], F32)
                rf1_b, rf2_b = Buf(), Buf()
                pT = [self.sb(st2, "m_pT%d" % i, [128, T], BF16) for i in range(3)]
                pT_b = [Buf() for _ in range(3)]
                rl = self.sb(st2, "m_rl", [128, T], F32)
                rl_b = Buf()
                osb = [self.sb(st2, "m_osb%d" % i, [128, T], BF16) for i in range(2)]
                osb_b = [Buf(), Buf()]
                pA = 0
                pS = 0
                pTi = 0
                oi = 0
                psO, psO_b = self.ps[6], self.ps_b[6]
                psL, psL_b = self.ps[7], self.ps_b[7]
                for h in range(H):
                    hb = h % 2
                    c0 = h * 192
                    wu = w["wuq"]
                    def qap(lo, hi):
                        return wu[:, lo:hi].rearrange("(j p) c -> p j c", p=128)
                    wq, wq_b = self.wload([(0, 192, qap(c0, c0 + 192)), (192, 224, qap(c0 + 160, c0 + 192)), (224, 256, qap(c0 + 128, c0 + 160))], QC, 256)
                    wkv, wkv_b = self.wload_rows(w["wukv"], 0, KVC, h * 256, 256)
                    for tt in range(NT):
                        tok = slice(tt * T, (tt + 1) * T)
                        ps, psb = self.ps[pA % 4], self.ps_b[pA % 4]; pA += 1
                        for kc in range(QC):
                            P.add("pe", lambda e, kc=kc, ps=ps, wq=wq, tok=tok: e.matmul(ps[:], lhsT=wq[:, kc, 0:128], rhs=cqn[:, kc, tok], start=(kc == 0), stop=(kc == QC - 1)),
                                  reads=[wq_b[kc], cqn_b[tt]], writes=[psb])
                        self.emit_copy("act", qn[hb][:, tok], ps[:], reads=[psb], writes=[qn_b[hb][tt]])
                        p2, p2b = self.ps[pA % 4], self.ps_b[pA % 4]; pA += 1
                        for kc in range(QC):
                            P.add("pe", lambda e, kc=kc, ps=p2, wq=wq, tok=tok: e.matmul(ps[0:64, :], lhsT=wq[:, kc, 128:192], rhs=cqn[:, kc, tok], start=(kc == 0), stop=(kc == QC - 1)),
                                  reads=[wq_b[kc], cqn_b[tt]], writes=[p2b])
                        p3, p3b = self.ps[pA % 4], self.ps_b[pA % 4]; pA += 1
                        for kc in range(QC):
                            P.add("pe", lambda e, kc=kc, ps=p3, wq=wq, tok=tok: e.matmul(ps[0:64, :], lhsT=wq[:, kc, 192:256], rhs=cqn[:, kc, tok], start=(kc == 0), stop=(kc == QC - 1)),
                                  reads=[wq_b[kc], cqn_b[tt]], writes=[p3b])
                        P.add("dve", lambda e, p2=p2, tok=tok: e.tensor_tensor(out=rf1[:], in0=p2[0:64, :], in1=cos2[:, tok], op=ALU.mult),
                              reads=[p2b, rope_b], writes=[rf1_b])
                        P.add("dve", lambda e, p3=p3, tok=tok: e.tensor_tensor(out=rf2[:], in0=p3[0:64, :], in1=sinS[:, tok], op=ALU.mult),
                              reads=[p3b, rope_b], writes=[rf2_b])
                        P.add("pool", lambda e, hb=hb, tok=tok: e.tensor_tensor(out=qr[hb][:, tok], in0=rf1[:], in1=rf2[:], op=ALU.add),
                              reads=[rf1_b, rf2_b], writes=[qr_b[hb][tt]])
                        ps, psb = self.ps[pA % 4], self.ps_b[pA % 4]; pA += 1
                        for kc in range(KVC):
                            P.add("pe", lambda e, kc=kc, ps=ps, wkv=wkv, tok=tok: e.matmul(ps[:], lhsT=wkv[:, kc, 0:128], rhs=ckvn[:, kc, tok], start=(kc == 0), stop=(kc == KVC - 1)),
                                  reads=[wkv_b[kc], ckvn_b[tt]], writes=[psb])
                        self.emit_copy("dve", kn[hb][:, tok], ps[:], reads=[psb], writes=[kn_b[hb][tt]])
                        ps, psb = self.ps[pA % 4], self.ps_b[pA % 4]; pA += 1
                        for b4 in range(4):
                            tb = tt * 4 + b4
                            for kc in range(KVC):
                                P.add("pe", lambda e, kc=kc, ps=ps, wkv=wkv, tb=tb, b4=b4: e.matmul(
                                    ps[:, b4 * 128:(b4 + 1) * 128], lhsT=ckvn[:, kc, tb * 128:(tb + 1) * 128], rhs=wkv[:, kc, 128:256],
                                    start=(kc == 0), stop=(kc == KVC - 1)), reads=[wkv_b[kc], ckvn_b[tt]], writes=[psb])
                        self.emit_copy("act", vv[hb][:, tt * 4:(tt + 1) * 4, :], ps[:].rearrange("p (b d) -> p b d", b=4), reads=[psb], writes=[vv_b[hb][tt]])
                    for qt in range(NT):
                        nkb = 4 * qt + 4
                        pend = None
                        for kb in range(nkb):
                            i = kb - 4 * qt
                            cc0 = max(i, 0) * 128
                            kt = kb // 4
                            ps, psb = self.ps[4 + pS % 2], self.ps_b[4 + pS % 2]; pS += 1
                            qs = slice(qt * T + cc0, (qt + 1) * T)
                            ks = slice(kb * 128, (kb + 1) * 128)
                            P.add("pe", lambda e, ps=ps, hb=hb, ks=ks, qs=qs, cc0=cc0: e.matmul(ps[:, cc0:T], lhsT=kn[hb][:, ks], rhs=qn[hb][:, qs], start=True, stop=False),
                                  reads=[kn_b[hb][kt], qn_b[hb][qt]], writes=[psb])
                            P.add("pe", lambda e, ps=ps, hb=hb, ks=ks, qs=qs, cc0=cc0: e.matmul(ps[:, cc0:T], lhsT=krT[:, ks], rhs=qr[hb][:, qs], start=False, stop=True),
                                  reads=[krT_b[kt], qr_b[hb][qt]], writes=[psb])
                            pt, ptb = pT[pTi % 3], pT_b[pTi % 3]; pTi += 1
                            P.add("act", lambda e, ps=ps, pt=pt, cc0=cc0: e.activation(out=pt[:, cc0:T], in_=ps[:, cc0:T], func=AF.Exp, scale=scale),
                                  reads=[psb], writes=[ptb])
                            if i >= 0:
                                P.add("pool", lambda e, pt=pt, cc0=cc0: e.tensor_tensor(out=pt[:, cc0:cc0 + 128], in0=pt[:, cc0:cc0 + 128], in1=self.tri_bf[:], op=ALU.mult),
                                      reads=[ptb, self.const_b], writes=[ptb])

                            def pv(pt=pt, ptb=ptb, hb=hb, kb=kb, kt=kt, cc0=cc0, nkb=nkb):
                                P.add("pe", lambda e, pt=pt, hb=hb, kb=kb, cc0=cc0, nkb=nkb: e.matmul(psO[:, cc0:T], lhsT=vv[hb][:, kb, :], rhs=pt[:, cc0:T], start=(kb == 0), stop=(kb == nkb - 1)),
                                      reads=[vv_b[hb][kt], ptb], writes=[psO_b])
                                P.add("pe", lambda e, pt=pt, kb=kb, cc0=cc0, nkb=nkb: e.matmul(psL[:, cc0:T], lhsT=self.ones_bf[:], rhs=pt[:, cc0:T], start=(kb == 0), stop=(kb == nkb - 1)),
                                      reads=[ptb, self.const_b], writes=[psL_b])
                            if pend is not None:
                                pend()
                            pend = pv
                        pend()
                        P.add("dve", lambda e: e.reciprocal(out=rl[:], in_=psL[:]), reads=[psL_b], writes=[rl_b])
                        ot, otb = osb[oi % 2], osb_b[oi % 2]; oi += 1
                        P.add("dve", lambda e, ot=ot: e.tensor_tensor(out=ot[:], in0=psO[:], in1=rl[:], op=ALU.mult), reads=[psO_b, rl_b], writes=[otb])
                        P.add("sp", lambda e, ot=ot, h=h, qt=qt: e.dma_start(out=self.oscr[h * 128:(h + 1) * 128, qt * T:(qt + 1) * T], in_=ot[:]),
                              reads=[otb], writes=[oscr_b[h][qt]], is_dma=True)
                P.barrier()
        with contextlib.ExitStack() as st:
            self.alloc_wpool(st, 4)
            self.abuf = self.sb(st, "mc_abuf", [128, DC, T], BF16)
            self.abuf_b = [Buf() for _ in range(DC)]
            abuf = self.abuf
            self.macc = self.sb(st, "mc_macc", [128, DC, T], F32)
            self.macc_b = [Buf() for _ in range(DC)]
            for t in range(NT):
                tok = slice(t * T, (t + 1) * T)
                P.add("sp", lambda e, tok=tok: e.dma_start(out=abuf[:], in_=self.oscr[:, tok].rearrange("(c p) t -> p c t", p=128)),
                      reads=[oscr_b[h][t] for h in range(H)], writes=self.abuf_b, is_dma=True)
                self.proj_to_macc(w["wo"], self.abuf, self.abuf_b, DC)
                self.post_norm_residual(src_ap, which, t, g1, self.abuf, self.abuf_b)
            P.barrier()

    def build(self):
        nc = self.nc
        P = self.P
        with contextlib.ExitStack() as st:
            self.cst = self.sb(st, "cst", [128, 512], F32)
            self.vec = self.sb(st, "vec", [128, self.NV], F32)
            self.ones_bf = self.sb(st, "ones_bf", [128, 128], BF16)
            self.tri_bf = self.sb(st, "tri_bf", [128, 128], BF16)
            self.ident = self.cst[:, 0:128]
            self.tri_f = self.cst[:, 128:256]
            self.const_b = Buf("const")
            self.negpi = self.sb(st, "negpi", [128, 1], F32)
            self.rstd = self.sb(st, "rstd", [128, T], F32)
            self.rstd_b = Buf("rstd")
            cpg = min(2, self.DC)
            self.tmp = [self.sb(st, "tmp%d" % i, [128, cpg, T], F32) for i in range(2)]
            self.tmp_b = [Buf(), Buf()]
            self.tmp_i = 0
            self.st_i = 0
            self.ps = [st.enter_context(nc.psum_tensor("ps%d" % i, [128, 512], F32)) for i in range(8)]
            self.ps_b = [Buf("ps%d" % i) for i in range(8)]
            P.add("sp", lambda e: e.dma_start(out=self.cst[:], in_=self.cst_d), writes=[self.const_b], is_dma=True)
            P.add("sp", lambda e: e.dma_start(out=self.vec[:], in_=self.vec_d), writes=[self.const_b], is_dma=True)
            P.add("pool", lambda e: e.memset(self.ones_bf[:], 1.0), writes=[self.const_b])
            P.add("pool", lambda e: e.memset(self.negpi[:], -math.pi * (1.0 - 1e-6)), writes=[self.const_b])
            P.add("dve", lambda e: e.tensor_copy(out=self.tri_bf[:], in_=self.cst[:, 128:256]), reads=[self.const_b], writes=[self.const_b])
            P.barrier()
            src, which = self.xT, "x"
            NLk = len(self.kinds)
            for li, k in enumerate(self.kinds):
                last = (li == NLk - 1)
                has_mixer = k in ("g", "G", "m", "M")
                has_ffn = k not in ("G", "M")
                if has_mixer:
                    if has_ffn:
                        self.dst_ap, self.dwhich = self.hB, "B"
                    else:
                        self.dst_ap, self.dwhich = (self.outT, "o") if last else (self.hA, "A")
                    if k in ("g", "G"):
                        self.gmlp(li, src, which)
                    else:
                        self.mla(li, src, which)
                    src, which = self.dst_ap, self.dwhich
                if has_ffn:
                    self.dst_ap, self.dwhich = (self.outT, "o") if last else (self.hA, "A")
                    self.ffn(li, src, which)
                    src, which = self.dst_ap, self.dwhich
            P.emit()
        return nc


def make_consts():
    c = np.zeros((128, 512), np.float32)
    c[:, 0:128] = np.eye(128, dtype=np.float32)
    p = np.arange(128)
    c[:, 128:256] = (p[:, None] <= p[None, :]).astype(np.float32)
    invf = (10000.0 ** (-np.arange(0, 64, 2, dtype=np.float32) / np.float32(64))).astype(np.float32)
    c[:, 256] = (invf[p % 32].astype(np.float64) / (2 * math.pi)).astype(np.float32)
    c[:, 257] = np.float32(0.25)
    c[:, 258] = np.where((p % 64) < 32, np.float32(0.5), np.float32(0.0))
    return c


def colmajor(v):
    v = np.asarray(v, np.float32)
    return np.ascontiguousarray(v.reshape(-1, 128).T)


_PROG_CACHE = {}


def get_prog(cfg, kinds):
    key = (tuple(sorted(cfg.items())), tuple(kinds))
    if key not in _PROG_CACHE:
        b = Builder(cfg, kinds)
        _PROG_CACHE[key] = b.build()
    return _PROG_CACHE[key]


def run_layers(cfg, kinds, layer_ids, hT_list, inputs):
    D, S, FH, H, QL, KVL = (cfg[k] for k in ("D", "S", "FH", "H", "QL", "KVL"))
    DC = D // 128
    nc = get_prog(cfg, kinds)
    NL = len(kinds)
    vec = np.zeros((128, NL * 4 * DC + NL * (2 * DC + QL // 128 + KVL // 128)), np.float32)
    shared = {"cst": make_consts()}
    for li, i in enumerate(layer_ids):
        j = i // 2
        for n in range(4):
            vec[:, li * 4 * DC + n * DC: li * 4 * DC + (n + 1) * DC] = colmajor(inputs["norm_g"][i, n])
        xb = NL * 4 * DC + li * (2 * DC + QL // 128 + KVL // 128)
        if kinds[li] in ("g", "G"):
            vec[:, xb:xb + DC] = colmajor(inputs["gmlp_ln_g"][j])
            vec[:, xb + DC:xb + 2 * DC] = colmajor(inputs["gmlp_ln_b"][j])
            shared["win%d" % li] = np.asarray(inputs["gmlp_w_in"][j], np.float32)
            shared["wout%d" % li] = np.asarray(inputs["gmlp_w_out"][j], np.float32)
            shared["ws%d" % li] = np.asarray(inputs["gmlp_w_s"][j], np.float32)
            shared["bs%d" % li] = np.ascontiguousarray(np.broadcast_to(
                np.asarray(inputs["gmlp_b_s"][j], np.float32).reshape(1, -1), (128, DC * 128)))
        elif kinds[li] in ("m", "M"):
            vec[:, xb + 2 * DC:xb + 2 * DC + QL // 128] = colmajor(inputs["mla_q_norm_g"][j])
            vec[:, xb + 2 * DC + QL // 128:xb + 2 * DC + QL // 128 + KVL // 128] = colmajor(inputs["mla_kv_norm_g"][j])
            shared["wdqkv%d" % li] = np.asarray(inputs["mla_w_dqkv"][j], np.float32)
            shared["wuq%d" % li] = np.asarray(inputs["mla_w_uq"][j], np.float32)
            shared["wukv%d" % li] = np.asarray(inputs["mla_w_ukv"][j], np.float32)
            shared["wo%d" % li] = np.asarray(inputs["mla_w_o"][j], np.float32)
        shared["wgu%d" % li] = np.asarray(inputs["ffn_w_gate_up"][i], np.float32)
        shared["wdn%d" % li] = np.asarray(inputs["ffn_w_down"][i], np.float32)
    shared["vec"] = vec
    n = len(hT_list)
    in_maps = []
    for c in range(n):
        m = dict(shared)
        m["xT"] = hT_list[c]
        if "m" in kinds or "M" in kinds:
            m["pos"] = np.ascontiguousarray(np.broadcast_to(np.asarray(inputs["positions"][c], np.int32).reshape(1, S), (64, S)))
        in_maps.append(m)
    res = run_bass_kernel_spmd(nc, in_maps, core_ids=list(range(n)))
    import os
    if os.environ.get("KDEBUG"):
        global DBG
        DBG = [r["hscrB"] for r in res.results]
    return [r["outT"] for r in res.results]


FUSED = True


def kernel(**inputs):
    cfg = CFG_FULL
    x = np.asarray(inputs["x"], np.float32)
    B = x.shape[0]
    hT = [np.ascontiguousarray(x[b].T) for b in range(B)]
    kinds_all = ["g", "m", "g", "m"]
    if FUSED:
        hT = run_layers(cfg, kinds_all, [0, 1, 2, 3], hT, inputs)
    else:
        for i in range(4):
            hT = run_layers(cfg, [kinds_all[i]], [i], hT, inputs)
    return np.stack([h.T for h in hT], axis=0).astype(np.float32)
```
